# Optimizing a Trainium2 kernel written in Bass

```python
import math
import jax
import jax.numpy as jnp
from jax import lax
import numpy as np

D_MODEL = 1024
BATCH = 4
SEQ = 8192
DEPTH = 4

GRID_W = 64
CTX_LEN = 256
HEAD_DIM = 64
A_HEADS = 4
A_KV_HEADS = 2
A_WIDTH = A_HEADS * HEAD_DIM
A_KV_WIDTH = A_KV_HEADS * HEAD_DIM
HY_WIDTH = 256
HY_ORDER = 2
HY_IN = (HY_ORDER + 1) * HY_WIDTH
HY_BANDS = 16
HY_EMB_DIM = 1 + 2 * HY_BANDS
HY_HIDDEN = 64
HY_FILTER_CH = 2 * HY_ORDER * HY_WIDTH
HY_FAST_DECAY = 0.3
HY_SLOW_DECAY = 1.5
HY_TARGET = 1e-2
D_HEADS = 4
D_VALUE_DIM = 2 * HEAD_DIM
D_WIDTH = D_HEADS * D_VALUE_DIM
D_QK_WIDTH = D_HEADS * 2 * HEAD_DIM
MIX_WIDTH = A_WIDTH + HY_WIDTH + D_WIDTH
IN_SPLITS = (A_WIDTH, A_WIDTH + A_KV_WIDTH, A_WIDTH + 2 * A_KV_WIDTH, A_WIDTH + 2 * A_KV_WIDTH + HY_IN, A_WIDTH + 2 * A_KV_WIDTH + HY_IN + D_QK_WIDTH, A_WIDTH + 2 * A_KV_WIDTH + HY_IN + 2 * D_QK_WIDTH)
IN_WIDTH = A_WIDTH + 2 * A_KV_WIDTH + HY_IN + 2 * D_QK_WIDTH + D_WIDTH
D_FF = 2816
ROPE_THETA = 10000.0
Q_BLOCK = 128
EPS = 1e-6

kernel_name = 'hybrid_parallel_heads_flow_block'


def rms_norm(x, g):
    xf = x.astype(jnp.float32)
    y = xf * lax.rsqrt(jnp.mean(xf * xf, axis=-1, keepdims=True) + EPS)
    return (y * g.astype(jnp.float32)).astype(x.dtype)


def modulate(h, shift, scale):
    return h * (1.0 + scale) + shift


def dwconv3(u, w, b):
    up = jnp.pad(u, ((0, 0), (1, 1), (0, 0)))
    return up[:, :-2] * w[0] + up[:, 1:-1] * w[1] + up[:, 2:] * w[2] + b


def axial_rope_tables(rows, cols):
    n_freq = HEAD_DIM // 4
    inv = ROPE_THETA ** (-jnp.arange(n_freq, dtype=jnp.float32) / n_freq)
    ang = jnp.concatenate([rows[:, None] * inv, cols[:, None] * inv], axis=-1)
    return jnp.cos(ang), jnp.sin(ang)


def apply_rope(x, cos, sin):
    t = x.shape[1]
    bshape = (t,) + (1,) * (x.ndim - 3) + (cos.shape[-1],)
    c = cos.reshape(bshape)
    s = sin.reshape(bshape)
    xf = x.astype(jnp.float32)
    x1 = xf[..., 0::2]
    x2 = xf[..., 1::2]
    out = jnp.stack([x1 * c - x2 * s, x1 * s + x2 * c], axis=-1).reshape(x.shape)
    return out.astype(x.dtype)


def sweep_attention(q, k, v):
    b, hk, g, n, d = q.shape
    nb = n // Q_BLOCK
    qb = jnp.moveaxis(q.reshape(b, hk, g, nb, Q_BLOCK, d), 3, 0)
    scale = d ** -0.5

    def one_block(qi):
        s = jnp.einsum('bkgqd,bksd->bkgqs', qi, k, preferred_element_type=jnp.float32) * scale
        p = jax.nn.softmax(s, axis=-1)
        return jnp.einsum('bkgqs,bksd->bkgqd', p.astype(v.dtype), v)

    o = lax.map(one_block, qb)
    return jnp.moveaxis(o, 0, 3).reshape(b, hk, g, n, v.shape[-1])


def gqa_attention(q, k, v):
    b, n, h, d = q.shape
    hk = k.shape[2]
    qg = q.reshape(b, n, hk, h // hk, d).transpose(0, 2, 3, 1, 4)
    o = sweep_attention(qg, k.transpose(0, 2, 1, 3), v.transpose(0, 2, 1, 3))
    return o.transpose(0, 3, 1, 2, 4).reshape(b, n, h * v.shape[-1])


def diff_attention(q, k, v, lam):
    vv = v.transpose(0, 2, 1, 3)

    def component(j):
        qj = q[..., j, :].transpose(0, 2, 1, 3)[:, :, None]
        kj = k[..., j, :].transpose(0, 2, 1, 3)
        return sweep_attention(qj, kj, vv)[:, :, 0]

    o = component(0) - lam.astype(v.dtype) * component(1)
    return o.transpose(0, 2, 1, 3)


def hyena_filters(n, fw1, fb1, fw2, fb2, fw3, fb3, freq):
    pos = jnp.arange(n, dtype=jnp.float32)
    t = (pos / max(n - 1, 1))[:, None]
    w = 2.0 * math.pi * pos[:, None] / n
    bands = jnp.linspace(1e-4, HY_BANDS - 1, HY_BANDS, dtype=jnp.float32)
    z = jnp.concatenate([t, jnp.cos(bands * w), -jnp.sin(bands * w)], axis=-1)
    h = jnp.sin(freq * (z.astype(fw1.dtype) @ fw1 + fb1))
    h = jnp.sin(freq * (h @ fw2 + fb2))
    h = (h @ fw3 + fb3).astype(jnp.float32)
    max_decay = math.log(HY_TARGET) / HY_FAST_DECAY
    min_decay = math.log(HY_TARGET) / HY_SLOW_DECAY
    deltas = jnp.linspace(min_decay, max_decay, HY_WIDTH, dtype=jnp.float32)
    decay = jnp.exp(-t * jnp.abs(deltas))
    h = h.reshape(n, 2, HY_ORDER, HY_WIDTH) * decay[:, None, None, :]
    return h / (jnp.sum(jnp.abs(h), axis=(0, 1), keepdims=True) + EPS)


def long_conv(u, h_fwd, h_bwd, bias):
    n = u.shape[1]
    k = jnp.concatenate([h_fwd, jnp.zeros_like(h_fwd[:1]), h_bwd[:0:-1]], axis=0)
    k_f = jnp.fft.rfft(k, n=2 * n, axis=0)
    u32 = u.astype(jnp.float32)
    y = jnp.fft.irfft(jnp.fft.rfft(u32, n=2 * n, axis=1) * k_f[None], n=2 * n, axis=1)[:, :n]
    return (y + u32 * bias.astype(jnp.float32)).astype(u.dtype)


def hyena(u, conv_w, conv_b, fw1, fb1, fw2, fb2, fw3, fb3, freq, bias):
    n = u.shape[1]
    u = dwconv3(u, conv_w, conv_b)
    v, x1, x2 = jnp.split(u, 3, axis=-1)
    h = hyena_filters(n, fw1, fb1, fw2, fb2, fw3, fb3, freq)
    z = x1 * long_conv(v, h[:, 0, 0], h[:, 1, 0], bias[0])
    return x2 * long_conv(z, h[:, 0, 1], h[:, 1, 1], bias[1])


def merge_groups(ya, yh, yd, g_out, w_out, lam_init):
    b, n = ya.shape[:2]
    ya = rms_norm(ya, g_out[:A_WIDTH])
    yh = rms_norm(yh, g_out[A_WIDTH:A_WIDTH + HY_WIDTH])
    yd = rms_norm(yd, g_out[A_WIDTH + HY_WIDTH:].reshape(D_HEADS, D_VALUE_DIM)) * (1.0 - lam_init)
    return jnp.concatenate([ya, yh, yd.reshape(b, n, D_WIDTH)], axis=-1) @ w_out


def token_mixers(h, hc, cos, sin, lam, lam_init, need_ctx, w_in, qn_a, kn_a, qn_d, kn_d, hy_conv_w, hy_conv_b, hy_fw1, hy_fb1, hy_fw2, hy_fb2, hy_fw3, hy_fb3, hy_freq, hy_bias, g_out, w_out):
    b, n = h.shape[:2]
    m = hc.shape[1]
    qa, ka, va, hy, qd, kd, vd = jnp.split(h @ w_in, IN_SPLITS, axis=-1)
    qa_c, ka_c, va_c, hy_c, qd_c, kd_c, vd_c = jnp.split(hc @ w_in, IN_SPLITS, axis=-1)
    ka = apply_rope(rms_norm(ka.reshape(b, n, A_KV_HEADS, HEAD_DIM), kn_a), cos, sin)
    va = va.reshape(b, n, A_KV_HEADS, HEAD_DIM)
    ka_c = rms_norm(ka_c.reshape(b, m, A_KV_HEADS, HEAD_DIM), kn_a)
    va_c = va_c.reshape(b, m, A_KV_HEADS, HEAD_DIM)
    qa = apply_rope(rms_norm(qa.reshape(b, n, A_HEADS, HEAD_DIM), qn_a), cos, sin)
    ya = gqa_attention(qa, jnp.concatenate([ka_c, ka], axis=1), jnp.concatenate([va_c, va], axis=1))
    kd = apply_rope(rms_norm(kd.reshape(b, n, D_HEADS, 2, HEAD_DIM), kn_d), cos, sin)
    vd = vd.reshape(b, n, D_HEADS, D_VALUE_DIM)
    kd_c = rms_norm(kd_c.reshape(b, m, D_HEADS, 2, HEAD_DIM), kn_d)
    vd_c = vd_c.reshape(b, m, D_HEADS, D_VALUE_DIM)
    qd = apply_rope(rms_norm(qd.reshape(b, n, D_HEADS, 2, HEAD_DIM), qn_d), cos, sin)
    yd = diff_attention(qd, jnp.concatenate([kd_c, kd], axis=1), jnp.concatenate([vd_c, vd], axis=1), lam)
    yh = hyena(hy, hy_conv_w, hy_conv_b, hy_fw1, hy_fb1, hy_fw2, hy_fb2, hy_fw3, hy_fb3, hy_freq, hy_bias)
    y = merge_groups(ya, yh, yd, g_out, w_out, lam_init)
    if not need_ctx:
        return y, None
    qa_c = rms_norm(qa_c.reshape(b, m, A_HEADS, HEAD_DIM), qn_a)
    ya_c = gqa_attention(qa_c, ka_c, va_c)
    qd_c = rms_norm(qd_c.reshape(b, m, D_HEADS, 2, HEAD_DIM), qn_d)
    yd_c = diff_attention(qd_c, kd_c, vd_c, lam)
    yh_c = hyena(hy_c, hy_conv_w, hy_conv_b, hy_fw1, hy_fb1, hy_fw2, hy_fb2, hy_fw3, hy_fb3, hy_freq, hy_bias)
    return y, merge_groups(ya_c, yh_c, yd_c, g_out, w_out, lam_init)


def conv_glu(h, w_up, conv_w, conv_b, w_down):
    a, v = jnp.split(h @ w_up, 2, axis=-1)
    a = dwconv3(a, conv_w, conv_b)
    return (jax.nn.gelu(a, approximate=True) * v) @ w_down


def setup_inputs(seed: int = 0) -> dict:
    key = jax.random.key(seed)
    keys = iter(jax.random.split(key, 40))

    def nrm(shape, scale):
        return jax.random.normal(next(keys), shape, jnp.float32) * scale

    def gain(shape):
        return 1.0 + nrm(shape, 0.02)

    L = DEPTH
    return {
        'x': nrm((BATCH, SEQ, D_MODEL), 1.0),
        'c': nrm((BATCH, D_MODEL), 1.0),
        'ctx': nrm((BATCH, CTX_LEN, D_MODEL), 1.0),
        'c_ctx': nrm((D_MODEL,), 1.0),
        'norm1_g': gain((L, D_MODEL)),
        'norm2_g': gain((L, D_MODEL)),
        'w_mod': nrm((L, D_MODEL, 6 * D_MODEL), 0.5 * D_MODEL ** -0.5),
        'b_mod': nrm((L, 6 * D_MODEL), 0.02),
        'w_in': nrm((L, D_MODEL, IN_WIDTH), D_MODEL ** -0.5),
        'qn_a': gain((L, HEAD_DIM)),
        'kn_a': gain((L, HEAD_DIM)),
        'qn_d': gain((L, HEAD_DIM)),
        'kn_d': gain((L, HEAD_DIM)),
        'lam_q1': nrm((L, HEAD_DIM), 0.1),
        'lam_k1': nrm((L, HEAD_DIM), 0.1),
        'lam_q2': nrm((L, HEAD_DIM), 0.1),
        'lam_k2': nrm((L, HEAD_DIM), 0.1),
        'hy_conv_w': nrm((L, 3, HY_IN), 3 ** -0.5),
        'hy_conv_b': nrm((L, HY_IN), 0.02),
        'hy_fw1': nrm((L, HY_EMB_DIM, HY_HIDDEN), HY_EMB_DIM ** -0.5),
        'hy_fb1': nrm((L, HY_HIDDEN), 0.5),
        'hy_fw2': nrm((L, HY_HIDDEN, HY_HIDDEN), HY_HIDDEN ** -0.5),
        'hy_fb2': nrm((L, HY_HIDDEN), 0.5),
        'hy_fw3': nrm((L, HY_HIDDEN, HY_FILTER_CH), HY_HIDDEN ** -0.5),
        'hy_fb3': nrm((L, HY_FILTER_CH), 0.02),
        'hy_freq': gain((L, HY_HIDDEN)),
        'hy_bias': nrm((L, HY_ORDER, HY_WIDTH), 0.1),
        'g_out': gain((L, MIX_WIDTH)),
        'w_out': nrm((L, MIX_WIDTH, D_MODEL), MIX_WIDTH ** -0.5),
        'w_up': nrm((L, D_MODEL, 2 * D_FF), D_MODEL ** -0.5),
        'ffn_conv_w': nrm((L, 3, D_FF), 3 ** -0.5),
        'ffn_conv_b': nrm((L, D_FF), 0.02),
        'w_down': nrm((L, D_FF, D_MODEL), D_FF ** -0.5),
    }


def reference(x, c, ctx, c_ctx, norm1_g, norm2_g, w_mod, b_mod, w_in, qn_a, kn_a, qn_d, kn_d, lam_q1, lam_k1, lam_q2, lam_k2, hy_conv_w, hy_conv_b, hy_fw1, hy_fb1, hy_fw2, hy_fb2, hy_fw3, hy_fb3, hy_freq, hy_bias, g_out, w_out, w_up, ffn_conv_w, ffn_conv_b, w_down):
    n_tok = x.shape[1]
    n_rows = n_tok // GRID_W
    rows = jnp.repeat(jnp.arange(n_rows, dtype=jnp.float32), GRID_W)
    cols = jnp.tile(jnp.arange(GRID_W, dtype=jnp.float32), n_rows)
    cos, sin = axial_rope_tables(rows, cols)
    s_lat = jax.nn.silu(c)
    s_ctx = jax.nn.silu(c_ctx)
    for i in range(DEPTH):
        need_ctx = i < DEPTH - 1
        lam_init = 0.8 - 0.6 * math.exp(-0.3 * i)
        lam = (jnp.exp(jnp.sum((lam_q1[i] * lam_k1[i]).astype(jnp.float32)))
               - jnp.exp(jnp.sum((lam_q2[i] * lam_k2[i]).astype(jnp.float32))) + lam_init)
        sh1, sc1, g1, sh2, sc2, g2 = jnp.split((s_lat @ w_mod[i] + b_mod[i])[:, None, :], 6, axis=-1)
        csh1, csc1, cg1, csh2, csc2, cg2 = jnp.split(s_ctx @ w_mod[i] + b_mod[i], 6, axis=-1)
        h = modulate(rms_norm(x, norm1_g[i]), sh1, sc1)
        hc = modulate(rms_norm(ctx, norm1_g[i]), csh1, csc1)
        y, yc = token_mixers(h, hc, cos, sin, lam, lam_init, need_ctx, w_in[i], qn_a[i], kn_a[i], qn_d[i], kn_d[i], hy_conv_w[i], hy_conv_b[i], hy_fw1[i], hy_fb1[i], hy_fw2[i], hy_fb2[i], hy_fw3[i], hy_fb3[i], hy_freq[i], hy_bias[i], g_out[i], w_out[i])
        x = x + g1 * y
        h = modulate(rms_norm(x, norm2_g[i]), sh2, sc2)
        x = x + g2 * conv_glu(h, w_up[i], ffn_conv_w[i], ffn_conv_b[i], w_down[i])
        if need_ctx:
            ctx = ctx + cg1 * yc
            hc = modulate(rms_norm(ctx, norm2_g[i]), csh2, csc2)
            ctx = ctx + cg2 * conv_glu(hc, w_up[i], ffn_conv_w[i], ffn_conv_b[i], w_down[i])
    return x
```

```python
import math
from contextlib import ExitStack

import numpy as np
import concourse.bass as bass
import concourse.mybir as mybir
from concourse.bass_utils import run_bass_kernel_spmd

F32 = mybir.dt.float32
BF16 = mybir.dt.bfloat16
AF = mybir.ActivationFunctionType
ALU = mybir.AluOpType

L = 4
D = 1024
NCTX = 256
NLAT = 8192
NT = NCTX + NLAT
INW = 2816
DFF = 2816
EPS = 1e-6
NCORES = 4
LB = 259


def set_sizes(nlat):
    global NLAT, NT, BLOCKS, NP, FFN_BLOCKS, NQB
    NLAT = nlat
    NT = NCTX + NLAT
    NQB = NLAT // 512
    BLOCKS = [(0, 256)] + [(256 + 512 * j, 512) for j in range(NQB)]
    NP = LB + NLAT + 1
    FFN_BLOCKS = [(0, 258, 0, 0)]
    for b in range((NLAT + 509) // 510):
        ntok = min(510, NLAT - 510 * b)
        FFN_BLOCKS.append((258 + 510 * b, ntok + 2, 256 + 510 * b, 0))


set_sizes(8192)
N_DMA_SEMS = 16
SAME_ENGINE_SYNC = True


class Buf:
    __slots__ = ("name", "w", "r", "dram")

    def __init__(self, name="", dram=False):
        self.name = name
        self.w = {}
        self.r = {}
        self.dram = dram


class Sched:
    ENGS = ("pe", "act", "dve", "pool", "sp")

    def __init__(self, nc, stack):
        self.nc = nc
        self.ops = {e: [] for e in self.ENGS}
        self.n_comp = {e: 0 for e in self.ENGS}
        self.n_dma = {e: 0 for e in self.ENGS}
        self.waited = {e: {} for e in self.ENGS}
        self.csem = {e: stack.enter_context(nc.semaphore("c_" + e)) for e in self.ENGS}
        self.dsem = {
            e: [stack.enter_context(nc.semaphore("d_%s_%d" % (e, i))) for i in range(N_DMA_SEMS)]
            for e in ("sp", "act", "pool")
        }
        self.dval = {e: [0] * N_DMA_SEMS for e in ("sp", "act", "pool")}

    def _waits(self, eng, reads, writes, is_dma):
        toks = {}
        for r in reads:
            for k, v in r.w.items():
                if toks.get(k, (None, 0))[1] < v[1]:
                    toks[k] = v
        for w in writes:
            for dct in ((w.r,) if w.dram else (w.w, w.r)):
                for k, v in dct.items():
                    if toks.get(k, (None, 0))[1] < v[1]:
                        toks[k] = v
        waits = []
        wd = self.waited[eng]
        own = id(self.csem[eng])
        for k, (sem, val) in toks.items():
            if k == own and not is_dma and not SAME_ENGINE_SYNC:
                continue
            if wd.get(k, 0) >= val:
                continue
            wd[k] = val
            waits.append((sem, val))
        return waits

    def _post(self, tok, reads, writes):
        k = id(tok[0])
        for r in reads:
            if r.r.get(k, (None, 0))[1] < tok[1]:
                r.r[k] = tok
        for w in writes:
            if w.dram:
                if w.w.get(k, (None, 0))[1] < tok[1]:
                    w.w[k] = tok
            else:
                w.w = {k: tok}
                w.r = {}

    def op(self, eng, fn, reads=(), writes=()):
        waits = self._waits(eng, reads, writes, False)
        self.n_comp[eng] += 1
        tok = (self.csem[eng], self.n_comp[eng])
        self.ops[eng].append((waits, fn, self.csem[eng], 1))
        self._post(tok, reads, writes)

    def dma(self, eng, fn, reads=(), writes=()):
        waits = self._waits(eng, reads, writes, True)
        j = self.n_dma[eng]
        self.n_dma[eng] += 1
        i = j % N_DMA_SEMS
        sem = self.dsem[eng][i]
        self.dval[eng][i] += 16
        tok = (sem, self.dval[eng][i])
        self.ops[eng].append((waits, fn, sem, 16))
        self._post(tok, reads, writes)

    def flush(self):
        nc = self.nc
        ops = self.ops
        self.ops = {e: [] for e in self.ENGS}
        tails = {}
        for e in ("sp", "act", "pool"):
            tails[e] = [(self.dsem[e][i], self.dval[e][i]) for i in range(N_DMA_SEMS) if self.dval[e][i] > 0]

        def run(engine, lst, extra=()):
            for waits, fn, sem, inc in lst:
                for (s, v) in waits:
                    engine.wait_ge(s, v)
                fn(engine).then_inc(sem, inc)
            for (s, v) in extra:
                engine.wait_ge(s, v)

        with nc.Block() as block:
            @block.tensor
            def _(e):
                run(e, ops["pe"])

            @block.scalar
            def _(e):
                run(e, ops["act"], tails["act"])

            @block.vector
            def _(e):
                run(e, ops["dve"])

            @block.gpsimd
            def _(e):
                run(e, ops["pool"], tails["pool"])

            @block.sync
            def _(e):
                run(e, ops["sp"], tails["sp"])


class Tl:
    def __init__(self, t, name):
        self.t = t
        self.b = Buf(name)


class Ctx:
    pass


def build(n_layers=L, dbg=(), with_hyena=True, stop_after=None):
    nc = bass.Bass("TRN2", target_bir_lowering=False)
    G = Ctx()

    def din(name, shape, dt=F32):
        return nc.dram_tensor(name, list(shape), dt, kind="ExternalInput").ap()

    def dscr(name, shape, dt):
        kind = "ExternalOutput" if name in dbg else "Internal"
        return nc.dram_tensor(name, list(shape), dt, kind=kind).ap()

    x_in = din("x", [NLAT, D])
    ctx_in = din("ctx", [NCTX, D])
    cvec = din("cvec", [128, 8, 2])
    ng_in = din("ng", [128, L, 2, 8])
    gout_in = din("gout", [128, L, 8])
    bmod_in = din("bmod", [1, L * 6 * D])
    qkg_in = din("qkg", [128, L, 4])
    lamv_in = din("lamv", [64, L, 4])
    fcw_in = din("fcw", [128, L, 3, 22])
    fcb_in = din("fcb", [128, L, 22])
    w_mod = din("w_mod", [n_layers, D, 6 * D])
    w_in = din("w_in", [n_layers, D, INW])
    w_out = din("w_out", [n_layers, D, D])
    w_up = din("w_up", [n_layers, D, 2 * DFF])
    w_down = din("w_down", [n_layers, DFF, D])
    ident_in = din("ident", [128, 128])
    onesbd_in = din("onesbd", [128, 128])
    rot_in = din("rot", [128, 128])
    cos_in = din("cos_t", [128, NT])
    sin_in = din("sin_t", [128, NT])
    out = nc.dram_tensor("out", [NLAT, D], F32, kind="ExternalOutput").ap()
    HC = {}
    if with_hyena:
        hp64_in = din("hp64", [64, L, 3])
        fb3_in = din("fb3_64", [64, L, 16])
        hb64_in = din("hb64", [64, L, 8])
        hcw_in = din("hcw64", [64, L, 3, 12])
        hcb_in = din("hcb64", [64, L, 12])
        dsc_in = din("dsc64", [64, 4])
        ficat_in = din("ficat", [128, 2, 256])
        hy_fw1 = din("hy_fw1", [n_layers, 33, 64])
        hy_fw2 = din("hy_fw2", [n_layers, 64, 64])
        hy_fw3 = din("hy_fw3", [n_layers, 64, 1024])
        for n_ in (NLAT, NCTX):
            R_ = n_ // 64
            A_ = n_ // 128
            M_ = 2 * n_
            c_ = Ctx()
            c_.n, c_.R, c_.A, c_.M = n_, R_, A_, M_
            c_.zext = din("zext%d" % n_, [33, M_])
            c_.tpos = din("tpos%d" % n_, [1, M_])
            c_.f1cat = din("f1cat%d" % n_, [R_, 2 * R_])
            c_.Gm = din("Gm%d" % n_, [128, R_, 3, 128])
            c_.Hm = din("Hm%d" % n_, [R_, 128, 2, A_])
            c_.GB = dscr("GB%d" % n_, [128, R_, 3, 128], BF16)
            c_.HB = dscr("HB%d" % n_, [R_, 128, 2, A_], BF16)
            c_.KSIG = dscr("KSIG%d" % n_, [8, 64, M_], BF16)
            c_.KF = dscr("KF%d" % n_, [8, 128, R_, 4, 64], BF16)
            c_.SIG = [dscr("SIG%d_%d" % (i, n_), [64, n_], BF16) for i in range(2)]
            c_.DWC = dscr("DWC%d" % n_, [3, 64, n_], F32)
            c_.ZF = dscr("ZF%d" % n_, [64, n_], F32)
            HC[n_] = c_

    XT = dscr("XT", [D, NT], F32)
    QT = dscr("QT", [768, NT], BF16)
    KT = dscr("KT", [640, NT], BF16)
    VV = dscr("VV", [NT, 640], BF16)
    HY = dscr("HY", [768, NT], F32)
    YT = dscr("YT", [D, NT], F32)
    H2T = dscr("H2T", [D, NP], BF16)
    UT = dscr("UT", [DFF, NP], BF16)
    XTv = XT.rearrange("(k p) t -> p k t", p=128)
    YTv = YT.rearrange("(k p) t -> p k t", p=128)
    H2Tv = H2T.rearrange("(k p) t -> p k t", p=128)
    UTv = UT.rearrange("(k p) t -> p k t", p=128)
    VVv = VV.rearrange("(t p) c -> p t c", p=128)

    dbufs = {}

    def db(name, c0=None, c1=None):
        if c0 is None:
            key = (name, -1)
            if key not in dbufs:
                dbufs[key] = Buf(name, True)
            return [dbufs[key]]
        res = []
        for i in range(c0 // 512, (c1 - 1) // 512 + 1):
            key = (name, i)
            if key not in dbufs:
                dbufs[key] = Buf(name, True)
            res.append(dbufs[key])
        return res

    with ExitStack() as gst:
        S = Sched(nc, gst)

        uid = [0]

        def sb(st, name, shape, dt):
            uid[0] += 1
            nm = "s%d_%s" % (uid[0], name)
            return Tl(st.enter_context(nc.sbuf_tensor(nm, list(shape), dt)), nm)

        PS = [Tl(gst.enter_context(nc.psum_tensor("ps%d" % i, [128, 512], F32)), "ps%d" % i) for i in range(8)]

        def mm(out_ap, lhsT, rhs, start, stop, reads, writes):
            S.op("pe", lambda e: e.matmul(out_ap, lhsT=lhsT, rhs=rhs, start=start, stop=stop), reads, writes)

        def act(out_ap, in_ap, func, reads, writes, scale=1.0, bias=0.0):
            S.op("act", lambda e: e.activation(out=out_ap, in_=in_ap, func=func, bias=bias, scale=scale), reads, writes)

        def tt(eng, out_ap, in0, in1, op, reads, writes):
            S.op(eng, lambda e: e.tensor_tensor(out=out_ap, in0=in0, in1=in1, op=op), reads, writes)

        def ts(eng, out_ap, in0, s1, op0, reads, writes, s2=None, op1=None):
            if op1 is None:
                S.op(eng, lambda e: e.tensor_scalar(out=out_ap, in0=in0, scalar1=s1, scalar2=None, op0=op0), reads, writes)
            else:
                S.op(eng, lambda e: e.tensor_scalar(out=out_ap, in0=in0, scalar1=s1, scalar2=s2, op0=op0, op1=op1), reads, writes)

        def stt(out_ap, in0, scalar, in1, op0, op1, reads, writes):
            S.op("dve", lambda e: e.scalar_tensor_tensor(out=out_ap, in0=in0, scalar=scalar, in1=in1, op0=op0, op1=op1), reads, writes)

        def cp(eng, out_ap, in_ap, reads, writes):
            if eng == "act":
                S.op("act", lambda e: e.copy(out=out_ap, in_=in_ap), reads, writes)
            else:
                S.op(eng, lambda e: e.tensor_copy(out=out_ap, in_=in_ap), reads, writes)

        def dma(q, out_ap, in_ap, reads, writes, slow=False):
            if slow:
                S.dma(q, lambda e: e.dma_start(out=out_ap, in_=in_ap, allow_slow_non_contiguous=True), reads, writes)
            else:
                S.dma(q, lambda e: e.dma_start(out=out_ap, in_=in_ap), reads, writes)

        def rstd_from_psum(ps_ap, ln_t, out_t, n, scale, reads_ps):
            act(ln_t.t[:, :n], ps_ap, AF.Ln, [reads_ps], [ln_t.b], scale=scale, bias=EPS)
            act(out_t.t[:, :n], ln_t.t[:, :n], AF.Exp, [ln_t.b], [out_t.b], scale=-0.5)

        ident = sb(gst, "ident", [128, 128], F32)
        ones_bf = sb(gst, "ones_bf", [128, 128], BF16)
        ones_f = sb(gst, "ones_f", [128, 128], F32)
        onesbd = sb(gst, "onesbd", [128, 128], BF16)
        rotf = sb(gst, "rotf", [128, 128], F32)
        ctmp = sb(gst, "ctmp", [128, 128], F32)
        svec = sb(gst, "svec", [128, 8, 2], F32)
        ng = sb(gst, "ng", [128, L, 2, 8], F32)
        goutt = sb(gst, "goutt", [128, L, 8], F32)
        qkg = sb(gst, "qkg", [128, L, 4], F32)
        lamv = sb(gst, "lamv", [64, L, 4], F32)
        fcw = sb(gst, "fcw", [128, L, 3, 22], F32)
        fcb = sb(gst, "fcb", [128, L, 22], F32)
        zt = sb(gst, "zt", [128, 8, 2], BF16)
        modT = sb(gst, "modT", [128, 48, 2], F32)
        G1 = sb(gst, "G1", [128, 8, 2], F32)
        G2 = sb(gst, "G2", [128, 8, 2], F32)
        neglam = sb(gst, "neglam", [128, 1], F32)
        gout2 = sb(gst, "gout2", [128, 8], F32)
        RG = [sb(gst, "RG%d" % i, [128, 128], BF16) for i in range(4)]

        dma("sp", ident.t[:], ident_in[:, :], [], [ident.b])
        dma("sp", ctmp.t[:], onesbd_in[:, :], [], [ctmp.b])
        cp("dve", onesbd.t[:], ctmp.t[:], [ctmp.b], [onesbd.b])
        dma("sp", rotf.t[:], rot_in[:, :], [], [rotf.b])
        S.op("pool", lambda e: e.memset(ones_bf.t[:], 1.0), [], [ones_bf.b])
        S.op("pool", lambda e: e.memset(ones_f.t[:], 1.0), [], [ones_f.b])
        S.op("pool", lambda e: e.memset(zt.t[:], 0.0), [], [zt.b])
        dma("sp", svec.t[:], cvec[:, :, :], [], [svec.b])
        act(svec.t[:], svec.t[:], AF.Silu, [svec.b], [svec.b])
        dma("sp", ng.t[:], ng_in[:, :, :, :], [], [ng.b])
        dma("sp", goutt.t[:], gout_in[:, :, :], [], [goutt.b])
        dma("sp", qkg.t[:], qkg_in[:, :, :], [], [qkg.b])
        dma("sp", lamv.t[:], lamv_in[:, :, :], [], [lamv.b])
        dma("sp", fcw.t[:], fcw_in[:, :, :, :], [], [fcw.b])
        dma("sp", fcb.t[:], fcb_in[:, :, :], [], [fcb.b])
        for col in (0, 257, 258, NP - 1):
            dma("sp", H2Tv[:, :, col:col + 1], zt.t[:, :, 0:1], [zt.b], db("H2T", col, col + 1), slow=True)

        with ExitStack() as st:
            xin = [sb(st, "xin%d" % i, [128, D], F32) for i in range(2)]
            xo = [sb(st, "xo%d" % i, [128, 8, 128], F32) for i in range(2)]
            for ti in range(NT // 128):
                xi = xin[ti % 2]
                xo_ = xo[ti % 2]
                src = ctx_in[ti * 128:(ti + 1) * 128, :] if ti < 2 else x_in[(ti - 2) * 128:(ti - 1) * 128, :]
                dma("sp", xi.t[:], src, [], [xi.b])
                for half in range(2):
                    pt = PS[(ti % 2) * 2 + half]
                    for kk in range(4):
                        k = half * 4 + kk
                        S.op("pe", lambda e, pt=pt, kk=kk, k=k, xi=xi: e.transpose(
                            out=pt.t[:, kk * 128:(kk + 1) * 128], in_=xi.t[:, k * 128:(k + 1) * 128], identity=ident.t[:]),
                            [xi.b, ident.b], [pt.b])
                    cp("act" if half == 0 else "dve", xo_.t[:, half * 4:(half + 1) * 4, :],
                       pt.t[:, :].rearrange("p (k t) -> p k t", k=4), [pt.b], [xo_.b])
                dma("sp", XTv[:, :, ti * 128:(ti + 1) * 128], xo_.t[:], [xo_.b], db("XT", ti * 128, (ti + 1) * 128))
            S.flush()


        if with_hyena:
            AX = mybir.AxisListType.X
            hp64 = sb(gst, "hp64", [64, L, 3], F32)
            fb3t = sb(gst, "fb3t", [64, L, 16], F32)
            hb64 = sb(gst, "hb64", [64, L, 8], F32)
            hcw = sb(gst, "hcw", [64, L, 3, 12], F32)
            hcb = sb(gst, "hcb", [64, L, 12], F32)
            dsc = sb(gst, "dsc", [64, 4], F32)
            fic = sb(gst, "fic", [128, 2, 256], BF16)
            fbf = sb(gst, "fbf", [64, L, 2], F32)
            dma("sp", hp64.t[:], hp64_in[:, :, :], [], [hp64.b])
            dma("sp", fb3t.t[:], fb3_in[:, :, :], [], [fb3t.b])
            dma("sp", hb64.t[:], hb64_in[:, :, :], [], [hb64.b])
            dma("sp", hcw.t[:], hcw_in[:, :, :, :], [], [hcw.b])
            dma("sp", hcb.t[:], hcb_in[:, :, :], [], [hcb.b])
            dma("sp", dsc.t[:], dsc_in[:, :], [], [dsc.b])
            dma("pool", fic.t[:], ficat_in[:, :, :], [], [fic.b])
            for hc in HC.values():
                hc.f1c = sb(gst, "f1c%d" % hc.n, [hc.R, 2 * hc.R], BF16)
                dma("pool", hc.f1c.t[:], hc.f1cat[:, :], [], [hc.f1c.b])
            with ExitStack() as st:
                gf = [sb(st, "gf%d" % i, [128, 4, 3, 128], F32) for i in range(2)]
                gb = [sb(st, "gb%d" % i, [128, 4, 3, 128], BF16) for i in range(2)]
                hf_ = [sb(st, "hf%d" % i, [128, 8, 2, 64], F32) for i in range(2)]
                hb_ = [sb(st, "hb%d" % i, [128, 8, 2, 64], BF16) for i in range(2)]
                ci = 0
                for hc in HC.values():
                    R, A = hc.R, hc.A
                    for r0 in range(0, R, 4):
                        a_, b_ = gf[ci % 2], gb[ci % 2]
                        ci += 1
                        dma("sp", a_.t[:], hc.Gm[:, r0:r0 + 4, :, :], [], [a_.b])
                        cp("dve", b_.t[:], a_.t[:], [a_.b], [b_.b])
                        dma("act", hc.GB[:, r0:r0 + 4, :, :], b_.t[:], [b_.b], db("GB%d" % hc.n))
                    for t0_ in range(0, 128, 8):
                        a_, b_ = hf_[ci % 2], hb_[ci % 2]
                        ci += 1
                        dma("sp", a_.t[0:R, :, :, 0:A], hc.Hm[:, t0_:t0_ + 8, :, :], [], [a_.b])
                        cp("dve", b_.t[0:R, :, :, 0:A], a_.t[0:R, :, :, 0:A], [a_.b], [b_.b])
                        dma("act", hc.HB[:, t0_:t0_ + 8, :, :], b_.t[0:R, :, :, 0:A], [b_.b], db("HB%d" % hc.n))
                S.flush()

        for l in range(0 if stop_after == "pro" else n_layers):
            need_ctx = l < L - 1
            lam_init = 0.8 - 0.6 * math.exp(-0.3 * l)

            with ExitStack() as st:
                wm = [sb(st, "wm%d" % i, [128, 8, 512], F32) for i in range(2)]
                modrow = sb(st, "modrow", [2, 6 * D], F32)
                bmodr = sb(st, "bmodr", [1, 6 * D], F32)
                lprod = sb(st, "lprod", [64, 2], F32)
                lexp = sb(st, "lexp", [128, 2], F32)
                dma("sp", bmodr.t[:], bmod_in[:, l * 6 * D:(l + 1) * 6 * D], [], [bmodr.b])
                for cb in range(12):
                    w_ = wm[cb % 2]
                    dma("sp" if cb % 2 == 0 else "act", w_.t[:],
                        w_mod[l].rearrange("(k p) n -> p k n", p=128)[:, :, cb * 512:(cb + 1) * 512], [], [w_.b])
                    pm = PS[cb % 2]
                    for k in range(8):
                        mm(pm.t[0:2, :], svec.t[:, k, :], w_.t[:, k, :], k == 0, False, [svec.b, w_.b], [pm.b])
                    mm(pm.t[0:2, :], ones_f.t[0:1, 0:2], bmodr.t[0:1, cb * 512:(cb + 1) * 512], False, True,
                       [ones_f.b, bmodr.b], [pm.b])
                    cp("dve", modrow.t[0:2, cb * 512:(cb + 1) * 512], pm.t[0:2, :], [pm.b], [modrow.b])
                pmt = PS[2]
                for blk in range(48):
                    S.op("pe", lambda e, blk=blk: e.transpose(
                        out=pmt.t[:, blk * 2:blk * 2 + 2], in_=modrow.t[0:2, blk * 128:(blk + 1) * 128],
                        identity=ident.t[0:2, 0:2]), [modrow.b, ident.b], [pmt.b])
                cp("dve", modT.t[:], pmt.t[:, 0:96].rearrange("p (b j) -> p b j", j=2), [pmt.b], [modT.b])
                for (Gt, nidx, scb) in ((G1, 0, 8), (G2, 1, 32)):
                    for j in range(2):
                        ts("dve", Gt.t[:, :, j], modT.t[:, scb:scb + 8, j], 1.0, ALU.add, [modT.b], [Gt.b])
                        tt("dve", Gt.t[:, :, j], Gt.t[:, :, j], ng.t[:, l, nidx, :], ALU.mult, [Gt.b, ng.b], [Gt.b])
                tt("dve", lprod.t[:, 0:1], lamv.t[:, l, 0:1], lamv.t[:, l, 1:2], ALU.mult, [lamv.b], [lprod.b])
                tt("dve", lprod.t[:, 1:2], lamv.t[:, l, 2:3], lamv.t[:, l, 3:4], ALU.mult, [lamv.b], [lprod.b])
                pl = PS[3]
                mm(pl.t[:, 0:2], ones_f.t[0:64, :], lprod.t[:, :], True, True, [ones_f.b, lprod.b], [pl.b])
                act(lexp.t[:], pl.t[:, 0:2], AF.Exp, [pl.b], [lexp.b])
                tt("dve", neglam.t[:], lexp.t[:, 1:2], lexp.t[:, 0:1], ALU.subtract, [lexp.b], [neglam.b])
                ts("dve", neglam.t[:], neglam.t[:], -lam_init, ALU.add, [neglam.b], [neglam.b])
                cp("dve", gout2.t[:, 0:4], goutt.t[:, l, 0:4], [goutt.b], [gout2.b])
                ts("dve", gout2.t[:, 4:8], goutt.t[:, l, 4:8], 1.0 - lam_init, ALU.mult, [goutt.b, gout2.b], [gout2.b])
                for gi in range(4):
                    ts("dve", RG[gi].t[:], rotf.t[:], qkg.t[:, l, gi:gi + 1], ALU.mult, [rotf.b, qkg.b], [RG[gi].b])
                S.flush()
            if stop_after == "mod":
                break

            def norm_mod(xb, n, j, Gt, shb, sq, lnt, rstd, tmp, hT, ps_ss):
                act(sq.t[:, :, :n], xb.t[:, :, :n], AF.Square, [xb.b], [sq.b])
                for k in range(8):
                    mm(ps_ss.t[:, :n], ones_bf.t[:], sq.t[:, k, :n], k == 0, k == 7, [ones_bf.b, sq.b], [ps_ss.b])
                rstd_from_psum(ps_ss.t[:, :n], lnt, rstd, n, 1.0 / D, ps_ss.b)
                for k in range(8):
                    t_ = tmp[k % 2]
                    stt(t_.t[:, :n], xb.t[:, k, :n], Gt.t[:, k, j:j + 1], rstd.t[:, :n], ALU.mult, ALU.mult,
                        [xb.b, Gt.b, rstd.b], [t_.b])
                    act(hT.t[:, k, :n], t_.t[:, :n], AF.Identity, [t_.b, modT.b], [hT.b],
                        bias=modT.t[:, shb + k, j:j + 1])

            with ExitStack() as st:
                win = sb(st, "win", [128, 8, INW], BF16)
                for k in range(8):
                    for c0, c1 in ((0, 1024), (1024, 2048), (2048, INW)):
                        dma("pool", win.t[:, k, c0:c1], w_in[l, k * 128:(k + 1) * 128, c0:c1], [], [win.b])
                xb = [sb(st, "xb%d" % i, [128, 8, 512], F32) for i in range(2)]
                sq = sb(st, "sq", [128, 8, 512], BF16)
                hT = [sb(st, "hT%d" % i, [128, 8, 512], BF16) for i in range(2)]
                tmp = [sb(st, "tmp%d" % i, [128, 512], F32) for i in range(2)]
                lnt = sb(st, "lnt", [128, 512], F32)
                rstd = sb(st, "rstd", [128, 512], F32)
                cosb = [sb(st, "cosb%d" % i, [128, 512], F32) for i in range(2)]
                sinb = [sb(st, "sinb%d" % i, [128, 512], F32) for i in range(2)]
                sqq = [sb(st, "sqq%d" % i, [128, 512], BF16) for i in range(2)]
                qb = [sb(st, "qb%d" % i, [128, 512], BF16) for i in range(2)]
                lnq = [sb(st, "lnq%d" % i, [128, 512], F32) for i in range(2)]
                rq = [sb(st, "rq%d" % i, [128, 512], F32) for i in range(2)]
                t1 = [sb(st, "t1_%d" % i, [128, 512], F32) for i in range(2)]
                t2 = [sb(st, "t2_%d" % i, [128, 512], F32) for i in range(2)]
                qo = [sb(st, "qo%d" % i, [128, 512], BF16) for i in range(2)]
                hyo = [sb(st, "hyo%d" % i, [128, 512], F32) for i in range(2)]
                vt = [sb(st, "vt%d" % i, [128, 640], BF16) for i in range(2)]
                fblocks = [(0, "q", 0, 0), (128, "q", 128, 0), (256, "k", 0, 1)]
                fblocks += [(512 + 128 * j, "hy", 128 * j, -1) for j in range(6)]
                fblocks += [(1280 + 128 * h, "q", 256 + 128 * h, 2) for h in range(4)]
                fblocks += [(1792 + 128 * h, "k", 128 + 128 * h, 3) for h in range(4)]
                cnt = 0
                vcnt = 0
                for bi, (t0, n) in enumerate(BLOCKS):
                    j = 1 if t0 == 0 else 0
                    xb_ = xb[bi % 2]
                    hT_ = hT[bi % 2]
                    dma("sp", xb_.t[:, :, :n], XTv[:, :, t0:t0 + n], db("XT", t0, t0 + n), [xb_.b])
                    dma("sp", cosb[bi % 2].t[:, :n], cos_in[:, t0:t0 + n], [], [cosb[bi % 2].b])
                    dma("sp", sinb[bi % 2].t[:, :n], sin_in[:, t0:t0 + n], [], [sinb[bi % 2].b])
                    norm_mod(xb_, n, j, G1, 0, sq, lnt, rstd, tmp, hT_, PS[0])
                    import os
                    SK = os.environ.get("SKIP", "")
                    for (c0, kind, r0, gi) in fblocks:
                        if "P" in SK or (kind == "hy" and "H" in SK) or (kind != "hy" and "Q" in SK):
                            continue
                        i2 = cnt % 2
                        cnt += 1
                        pq = PS[1 + i2]
                        for k in range(8):
                            mm(pq.t[:, :n], win.t[:, k, c0:c0 + 128], hT_.t[:, k, :n], k == 0, k == 7,
                               [win.b, hT_.b], [pq.b])
                        if kind == "hy":
                            ho = hyo[i2]
                            cp("act", ho.t[:, :n], pq.t[:, :n], [pq.b], [ho.b])
                            dma("sp", HY[r0:r0 + 128, t0:t0 + n], ho.t[:, :n], [ho.b], db("HY", t0, t0 + n))
                            continue
                        pss = PS[3 + i2]
                        prot = PS[5 + i2]
                        qf = hyo[i2]
                        cp("act", qf.t[:, :n], pq.t[:, :n], [pq.b], [qf.b])
                        act(sqq[i2].t[:, :n], qf.t[:, :n], AF.Square, [qf.b], [sqq[i2].b])
                        cp("dve", qb[i2].t[:, :n], qf.t[:, :n], [qf.b], [qb[i2].b])
                        mm(pss.t[:, :n], onesbd.t[:], sqq[i2].t[:, :n], True, True, [onesbd.b, sqq[i2].b], [pss.b])
                        mm(prot.t[:, :n], RG[gi].t[:], qb[i2].t[:, :n], True, True, [RG[gi].b, qb[i2].b], [prot.b])
                        rstd_from_psum(pss.t[:, :n], lnq[i2], rq[i2], n, 1.0 / 64, pss.b)
                        cp("dve", t2[i2].t[:, :n], prot.t[:, :n], [prot.b], [t2[i2].b])
                        stt(t1[i2].t[:, :n], qf.t[:, :n], qkg.t[:, l, gi:gi + 1], cosb[bi % 2].t[:, :n], ALU.mult, ALU.mult,
                            [qf.b, qkg.b, cosb[bi % 2].b], [t1[i2].b])
                        tt("dve", t2[i2].t[:, :n], t2[i2].t[:, :n], sinb[bi % 2].t[:, :n], ALU.mult,
                           [t2[i2].b, sinb[bi % 2].b], [t2[i2].b])
                        tt("dve", t1[i2].t[:, :n], t1[i2].t[:, :n], t2[i2].t[:, :n], ALU.add, [t1[i2].b, t2[i2].b], [t1[i2].b])
                        tt("dve", qo[i2].t[:, :n], t1[i2].t[:, :n], rq[i2].t[:, :n], ALU.mult, [t1[i2].b, rq[i2].b], [qo[i2].b])
                        dst = QT if kind == "q" else KT
                        dma("sp", dst[r0:r0 + 128, t0:t0 + n], qo[i2].t[:, :n], [qo[i2].b],
                            db("QT" if kind == "q" else "KT", t0, t0 + n))
                    for tti in range(n // 128):
                        if "V" in SK:
                            continue
                        v_ = vt[vcnt % 2]
                        vcnt += 1
                        pva, pvd = PS[0], PS[7]
                        for k in range(8):
                            mm(pva.t[:, 0:128], hT_.t[:, k, tti * 128:(tti + 1) * 128], win.t[:, k, 384:512], k == 0, k == 7,
                               [hT_.b, win.b], [pva.b])
                        for k in range(8):
                            mm(pvd.t[:, 0:512], hT_.t[:, k, tti * 128:(tti + 1) * 128], win.t[:, k, 2304:2816], k == 0, k == 7,
                               [hT_.b, win.b], [pvd.b])
                        cp("act", v_.t[:, 0:128], pva.t[:, 0:128], [pva.b], [v_.b])
                        cp("dve", v_.t[:, 128:640], pvd.t[:, 0:512], [pvd.b], [v_.b])
                        r = t0 + tti * 128
                        dma("sp", VV[r:r + 128, :], v_.t[:], [v_.b], db("VV", r, r + 128))
                S.flush()
            if stop_after == "AB":
                break

            units = []
            for h in range(4):
                units.append(dict(comps=[(h * 64, (h // 2) * 64)], vcol=(h // 2) * 64, dv=64, yrow=h * 64))
            for h in range(4):
                units.append(dict(comps=[(256 + h * 128, 128 + h * 128), (256 + h * 128 + 64, 128 + h * 128 + 64)],
                                  vcol=128 + h * 128, dv=128, yrow=512 + h * 128))
            with ExitStack() as st:
                KTu = [sb(st, "KTu%d" % i, [64, NT], BF16) for i in range(2)]
                QTu = [sb(st, "QTu%d" % i, [64, NT], BF16) for i in range(2)]
                Vu = sb(st, "Vu", [128, NT // 128, 128], BF16)
                PT = [sb(st, "PT%d" % i, [128, 512], BF16) for i in range(4)]
                Rr = [sb(st, "Rr%d" % i, [128, 512], F32) for i in range(2)]
                o0 = [sb(st, "o0_%d" % i, [128, 512], F32) for i in range(2)]
                o1 = [sb(st, "o1_%d" % i, [128, 512], F32) for i in range(2)]
                yo = [sb(st, "yo%d" % i, [128, 512], F32) for i in range(2)]
                qblocks = ([(0, 256, 2)] if need_ctx else []) + [(256 + 512 * jj, 512, NT // 128) for jj in range(NQB)]
                ptc = 0
                oc = 0
                for u in units:
                    dv = u["dv"]
                    nco = len(u["comps"])
                    for ci, (qr, kr) in enumerate(u["comps"]):
                        dma("sp", KTu[ci].t[:, :], KT[kr:kr + 64, :], db("KT", 0, NT), [KTu[ci].b])
                        dma("act", QTu[ci].t[:, :], QT[qr:qr + 64, :], db("QT", 0, NT), [QTu[ci].b])
                    dma("sp", Vu.t[:, :, 0:dv], VVv[:, :, u["vcol"]:u["vcol"] + dv], db("VV", 0, NT), [Vu.b])
                    its = []
                    for qi, (q0, nq, nch) in enumerate(qblocks):
                        for ci in range(nco):
                            for kc in range(nch):
                                its.append((qi, q0, nq, nch, ci, kc))

                    def emit_qk(idx):
                        (qi, q0, nq, nch, ci, kc) = its[idx]
                        pS = PS[idx % 2]
                        mm(pS.t[:, :nq], KTu[ci].t[0:64, kc * 128:(kc + 1) * 128], QTu[ci].t[0:64, q0:q0 + nq], True, True,
                           [KTu[ci].b, QTu[ci].b], [pS.b])

                    emit_qk(0)
                    for idx, (qi, q0, nq, nch, ci, kc) in enumerate(its):
                        if idx + 1 < len(its):
                            emit_qk(idx + 1)
                        pS = PS[idx % 2]
                        P_ = PT[ptc % 4]
                        ptc += 1
                        bset = 2 + 2 * ((qi * nco + ci) % 2) if nco == 1 else 2 + 2 * ci
                        pO, pSm = PS[bset], PS[bset + 1]
                        act(P_.t[:, :nq], pS.t[:, :nq], AF.Exp, [pS.b], [P_.b], scale=0.125)
                        mm(pO.t[0:dv, :nq], Vu.t[:, kc, 0:dv], P_.t[:, :nq], kc == 0, kc == nch - 1, [Vu.b, P_.b], [pO.b])
                        mm(pSm.t[0:dv, :nq], ones_bf.t[:, 0:dv], P_.t[:, :nq], kc == 0, kc == nch - 1, [ones_bf.b, P_.b], [pSm.b])
                        if kc == nch - 1:
                            R_ = Rr[oc % 2]
                            of_ = o1[oc % 2]
                            cp("dve", R_.t[0:dv, :nq], pSm.t[0:dv, :nq], [pSm.b], [R_.b])
                            cp("dve", of_.t[0:dv, :nq], pO.t[0:dv, :nq], [pO.b], [of_.b])
                            S.op("dve", lambda e, R_=R_, nq=nq, dv=dv: e.reciprocal(out=R_.t[0:dv, :nq], in_=R_.t[0:dv, :nq]),
                                 [R_.b], [R_.b])
                            if nco == 1:
                                y_ = yo[oc % 2]
                                tt("dve", y_.t[0:dv, :nq], of_.t[0:dv, :nq], R_.t[0:dv, :nq], ALU.mult, [of_.b, R_.b], [y_.b])
                                dma("sp", YT[u["yrow"]:u["yrow"] + dv, q0:q0 + nq], y_.t[0:dv, :nq], [y_.b], db("YT", q0, q0 + nq))
                            elif ci == 0:
                                o_ = o0[qi % 2]
                                tt("dve", o_.t[0:dv, :nq], of_.t[0:dv, :nq], R_.t[0:dv, :nq], ALU.mult, [of_.b, R_.b], [o_.b])
                            else:
                                o_ = o0[qi % 2]
                                y_ = yo[qi % 2]
                                tt("dve", of_.t[0:dv, :nq], of_.t[0:dv, :nq], R_.t[0:dv, :nq], ALU.mult, [of_.b, R_.b], [of_.b])
                                stt(y_.t[0:dv, :nq], of_.t[0:dv, :nq], neglam.t[0:dv, 0:1], o_.t[0:dv, :nq], ALU.mult, ALU.add,
                                    [of_.b, neglam.b, o_.b], [y_.b])
                                dma("sp", YT[u["yrow"]:u["yrow"] + dv, q0:q0 + nq], y_.t[0:dv, :nq], [y_.b], db("YT", q0, q0 + nq))
                            oc += 1
                S.flush()
            if stop_after == "C":
                break

            if with_hyena:
                PI = math.pi
                for hc in HC.values():
                    if hc.n == NCTX and not need_ctx:
                        continue
                    n, R, A, M = hc.n, hc.R, hc.A, hc.M
                    t0c = 0 if hc.n == NCTX else NCTX
                    cs = min(512, n)
                    nchk = M // cs
                    tg = "%d" % n

                    def forward(st, src, Kr, consume):
                        Asb = sb(st, "Asb", [128, 2, R, 64], BF16)
                        Uc = [sb(st, "Uc%d" % i, [Kr, 16, 128], BF16) for i in range(2)]
                        Gc = [sb(st, "Gc%d" % i, [128, 4, 3, 128], BF16) for i in range(2)]
                        Xs = [sb(st, "Xs%d" % i, [128, 128], F32) for i in range(2)]
                        for cc in range(4):
                            u_ = Uc[cc % 2]
                            dma("sp", u_.t[:], src[cc * 16:(cc + 1) * 16, :].rearrange("c (a b) -> a c b", b=128),
                                db("SRC" + tg), [u_.b])
                            for c2 in range(8):
                                pb = PS[c2 % 2]
                                for e_ in range(2):
                                    cl = c2 * 2 + e_
                                    mm(pb.t[:, e_ * 2 * R:(e_ + 1) * 2 * R], u_.t[0:Kr, cl, :], hc.f1c.t[0:Kr, :], True, True,
                                       [u_.b, hc.f1c.b], [pb.b])
                                c0_ = cc * 16 + c2 * 2
                                cp("act" if c2 % 2 == 0 else "dve",
                                   Asb.t[:, :, :, c0_:c0_ + 2].rearrange("b e r c -> b c e r"),
                                   pb.t[:, 0:4 * R].rearrange("b (c e r) -> b c e r", c=2, e=2), [pb.b], [Asb.b])
                        for r0 in range(0, R, 4):
                            g_ = Gc[(r0 // 4) % 2]
                            dma("act", g_.t[:], hc.GB[:, r0:r0 + 4, :, :], db("GB" + tg), [g_.b])
                            for rr in range(4):
                                r = r0 + rr
                                pb = PS[2 + r % 2]
                                mm(pb.t[:, 0:64], g_.t[:, rr, 0, :], Asb.t[:, 0, r, :], True, False, [g_.b, Asb.b], [pb.b])
                                mm(pb.t[:, 0:64], g_.t[:, rr, 2, :], Asb.t[:, 1, r, :], False, True, [g_.b, Asb.b], [pb.b])
                                mm(pb.t[:, 64:128], g_.t[:, rr, 1, :], Asb.t[:, 0, r, :], True, False, [g_.b, Asb.b], [pb.b])
                                mm(pb.t[:, 64:128], g_.t[:, rr, 0, :], Asb.t[:, 1, r, :], False, True, [g_.b, Asb.b], [pb.b])
                                x_ = Xs[r % 2]
                                cp("act", x_.t[:], pb.t[:, 0:128], [pb.b], [x_.b])
                                consume(r, x_)

                    with ExitStack() as st:
                        fw1b = sb(st, "fw1b", [33, 64], BF16)
                        fw2b = sb(st, "fw2b", [64, 64], BF16)
                        fw3b = sb(st, "fw3b", [64, 1024], BF16)
                        dma("pool", fw1b.t[:], hy_fw1[l], [], [fw1b.b])
                        dma("pool", fw2b.t[:], hy_fw2[l], [], [fw2b.b])
                        dma("pool", fw3b.t[:], hy_fw3[l], [], [fw3b.b])
                        tt("dve", fbf.t[:, l, 0:1], hp64.t[:, l, 1:2], hp64.t[:, l, 0:1], ALU.mult, [hp64.b], [fbf.b])
                        tt("dve", fbf.t[:, l, 1:2], hp64.t[:, l, 2:3], hp64.t[:, l, 0:1], ALU.mult, [hp64.b], [fbf.b])
                        H2all = sb(st, "H2all", [64, M], BF16)
                        zf32 = [sb(st, "zf32_%d" % i, [33, 512], F32) for i in range(2)]
                        zb = [sb(st, "zb%d" % i, [33, 512], BF16) for i in range(2)]
                        a1 = sb(st, "a1", [64, 512], F32)
                        wm_ = sb(st, "wm_", [64, 512], F32)
                        h1 = sb(st, "h1", [64, 512], BF16)

                        def sin_layer(ps_ap, fbcol, out_ap, reads_ps, out_b):
                            cp("act", a1.t[:, :cs], ps_ap, [reads_ps], [a1.b])
                            ts("dve", a1.t[:, :cs], a1.t[:, :cs], hp64.t[:, l, 0:1], ALU.mult, [a1.b, hp64.b, fbf.b], [a1.b],
                               s2=fbf.t[:, l, fbcol:fbcol + 1], op1=ALU.add)
                            ts("dve", wm_.t[:, :cs], a1.t[:, :cs], PI, ALU.is_gt, [a1.b], [wm_.b], s2=-2.0 * PI, op1=ALU.mult)
                            tt("dve", a1.t[:, :cs], a1.t[:, :cs], wm_.t[:, :cs], ALU.add, [a1.b, wm_.b], [a1.b])
                            ts("dve", wm_.t[:, :cs], a1.t[:, :cs], -PI, ALU.is_lt, [a1.b], [wm_.b], s2=2.0 * PI, op1=ALU.mult)
                            tt("dve", a1.t[:, :cs], a1.t[:, :cs], wm_.t[:, :cs], ALU.add, [a1.b, wm_.b], [a1.b])
                            act(out_ap, a1.t[:, :cs], AF.Sin, [a1.b], [out_b])

                        for ch in range(nchk):
                            zf_, zb_ = zf32[ch % 2], zb[ch % 2]
                            dma("sp", zf_.t[:, :cs], hc.zext[:, ch * cs:(ch + 1) * cs], [], [zf_.b])
                            cp("dve", zb_.t[:, :cs], zf_.t[:, :cs], [zf_.b], [zb_.b])
                            p1 = PS[ch % 2]
                            mm(p1.t[0:64, :cs], fw1b.t[:, :], zb_.t[:, :cs], True, True, [fw1b.b, zb_.b], [p1.b])
                            sin_layer(p1.t[0:64, :cs], 0, h1.t[:, :cs], p1.b, h1.b)
                            p2 = PS[2 + ch % 2]
                            mm(p2.t[0:64, :cs], fw2b.t[:, :], h1.t[:, :cs], True, True, [fw2b.b, h1.b], [p2.b])
                            sin_layer(p2.t[0:64, :cs], 1, H2all.t[:, ch * cs:(ch + 1) * cs], p2.b, H2all.b)

                        kT = sb(st, "kT", [64, M], F32)
                        kbf = sb(st, "kbf", [64, M], BF16)
                        tprow = [sb(st, "tprow%d" % i, [1, 512], F32) for i in range(2)]
                        dec = [sb(st, "dec%d" % i, [64, 512], F32) for i in range(2)]
                        kk = [sb(st, "kk%d" % i, [64, 512], F32) for i in range(2)]
                        asum = sb(st, "asum", [64, 64], F32)
                        tot = sb(st, "tot", [64, 1], F32)
                        for o in range(2):
                            for g in range(4):
                                idx = o * 4 + g
                                for ch in range(nchk):
                                    dr = 0 if ch * cs < n else 1
                                    blk = dr * 8 + o * 4 + g
                                    col0 = dr * 512 + o * 256 + g * 64
                                    p3, p4 = PS[4 + ch % 2], PS[6 + ch % 2]
                                    mm(p3.t[0:64, :cs], fw3b.t[:, col0:col0 + 64], H2all.t[:, ch * cs:(ch + 1) * cs], True, True,
                                       [fw3b.b, H2all.b], [p3.b])
                                    tp_ = tprow[ch % 2]
                                    dma("sp", tp_.t[:, :cs], hc.tpos[:, ch * cs:(ch + 1) * cs], [], [tp_.b])
                                    mm(p4.t[0:64, :cs], ones_f.t[0:1, 0:64], tp_.t[0:1, :cs], True, True, [ones_f.b, tp_.b], [p4.b])
                                    d_ = dec[ch % 2]
                                    k_ = kk[ch % 2]
                                    cp("dve", d_.t[:, :cs], p4.t[0:64, :cs], [p4.b], [d_.b])
                                    ts("dve", d_.t[:, :cs], d_.t[:, :cs], dsc.t[:, g:g + 1], ALU.mult, [d_.b, dsc.b], [d_.b])
                                    act(d_.t[:, :cs], d_.t[:, :cs], AF.Exp, [d_.b], [d_.b])
                                    act(k_.t[:, :cs], p3.t[0:64, :cs], AF.Identity, [p3.b, fb3t.b], [k_.b], bias=fb3t.t[:, l, blk:blk + 1])
                                    tt("dve", kT.t[:, ch * cs:(ch + 1) * cs], k_.t[:, :cs], d_.t[:, :cs], ALU.mult, [k_.b, d_.b], [kT.b])
                                    S.op("dve", lambda e, ch=ch: e.tensor_reduce(
                                        out=asum.t[:, ch:ch + 1], in_=kT.t[:, ch * cs:(ch + 1) * cs], axis=AX, op=ALU.add,
                                        apply_absolute_value=True), [kT.b], [asum.b])
                                S.op("dve", lambda e: e.tensor_reduce(out=tot.t[:], in_=asum.t[:, 0:nchk], axis=AX, op=ALU.add),
                                     [asum.b], [tot.b])
                                ts("dve", tot.t[:], tot.t[:], EPS, ALU.add, [tot.b], [tot.b])
                                S.op("dve", lambda e: e.reciprocal(out=tot.t[:], in_=tot.t[:]), [tot.b], [tot.b])
                                S.op("dve", lambda e: e.memset(kT.t[:, n:n + 1], 0.0), [kT.b], [kT.b])
                                ts("dve", kbf.t[:], kT.t[:], tot.t[:, 0:1], ALU.mult, [kT.b, tot.b], [kbf.b])
                                dma("sp", hc.KSIG[idx], kbf.t[:], [kbf.b], db("KSIG" + tg))
                        S.flush()
                    for idx in range(8):
                        with ExitStack() as st:
                            KFs = [sb(st, "KFs%d" % i, [128, 4, 4, 64], BF16) for i in range(2)]

                            def cons_f(r, x_, idx=idx, KFs=KFs):
                                kf_ = KFs[(r // 4) % 2]
                                rr = r % 4
                                cp("dve", kf_.t[:, rr, 0, :], x_.t[:, 0:64], [x_.b], [kf_.b])
                                cp("dve", kf_.t[:, rr, 1, :], x_.t[:, 0:64], [x_.b], [kf_.b])
                                cp("act", kf_.t[:, rr, 2, :], x_.t[:, 64:128], [x_.b], [kf_.b])
                                cp("act", kf_.t[:, rr, 3, :], x_.t[:, 64:128], [x_.b], [kf_.b])
                                if rr == 3:
                                    dma("sp", hc.KF[idx, :, r - 3:r + 1, :, :], kf_.t[:], [kf_.b], db("KF" + tg))

                            dbufs[("SRC" + tg, -1)] = db("KSIG" + tg)[0]
                            forward(st, hc.KSIG[idx], R, cons_f)
                            S.flush()

                    for g in range(4):
                        with ExitStack() as st:
                            ub = sb(st, "ub", [64, n + 2], F32)
                            c1 = sb(st, "c1", [64, n], F32)
                            c2 = sb(st, "c2", [64, n], F32)
                            vb = sb(st, "vb", [64, n], BF16)
                            S.op("dve", lambda e: e.memset(ub.t[:, 0:1], 0.0), [ub.b], [ub.b])
                            S.op("dve", lambda e: e.memset(ub.t[:, n + 1:n + 2], 0.0), [ub.b], [ub.b])
                            for kind in range(3):
                                r0_ = kind * 256 + g * 64
                                ti_ = kind * 4 + g
                                dma("sp", ub.t[:, 1:n + 1], HY[r0_:r0_ + 64, t0c:t0c + n], db("HY", t0c, t0c + n), [ub.b])
                                ts("dve", c1.t[:], ub.t[:, 1:n + 1], hcw.t[:, l, 1, ti_:ti_ + 1], ALU.mult, [ub.b, hcw.b, hcb.b], [c1.b],
                                   s2=hcb.t[:, l, ti_:ti_ + 1], op1=ALU.add)
                                stt(c2.t[:], ub.t[:, 0:n], hcw.t[:, l, 0, ti_:ti_ + 1], c1.t[:], ALU.mult, ALU.add, [ub.b, hcw.b, c1.b], [c2.b])
                                stt(c1.t[:], ub.t[:, 2:n + 2], hcw.t[:, l, 2, ti_:ti_ + 1], c2.t[:], ALU.mult, ALU.add, [ub.b, hcw.b, c2.b], [c1.b])
                                dma("sp", hc.DWC[kind], c1.t[:], [c1.b], db("DWC" + tg))
                                if kind == 0:
                                    cp("act", vb.t[:], c1.t[:], [c1.b], [vb.b])
                                    dma("sp", hc.SIG[0], vb.t[:], [vb.b], db("SIG0" + tg))
                            S.flush()
                        for o in range(2):
                            with ExitStack() as st:
                                idx = o * 4 + g
                                Y = sb(st, "Y", [128, 2, 64, R], BF16)
                                KFt = [sb(st, "KFt%d" % i, [128, 4, 4, 64], BF16) for i in range(2)]
                                T1 = [sb(st, "T1_%d" % i, [128, 128], F32) for i in range(2)]
                                T2 = [sb(st, "T2_%d" % i, [128, 128], F32) for i in range(2)]

                                def cons_s(r, x_, idx=idx, Y=Y, KFt=KFt, T1=T1, T2=T2):
                                    kf_ = KFt[(r // 4) % 2]
                                    rr = r % 4
                                    if rr == 0:
                                        dma("sp", kf_.t[:], hc.KF[idx, :, r:r + 4, :, :], db("KF" + tg), [kf_.b])
                                    a_, b_ = T1[r % 2], T2[r % 2]
                                    tt("dve", a_.t[:], x_.t[:], kf_.t[:, rr, 0:2, :].rearrange("p e c -> p (e c)"), ALU.mult, [x_.b, kf_.b], [a_.b])
                                    tt("dve", b_.t[:], x_.t[:], kf_.t[:, rr, 2:4, :].rearrange("p e c -> p (e c)"), ALU.mult, [x_.b, kf_.b], [b_.b])
                                    tt("dve", Y.t[:, 0, :, r], a_.t[:, 0:64], b_.t[:, 64:128], ALU.subtract, [a_.b, b_.b], [Y.b])
                                    tt("dve", Y.t[:, 1, :, r], b_.t[:, 0:64], a_.t[:, 64:128], ALU.add, [a_.b, b_.b], [Y.b])

                                dbufs[("SRC" + tg, -1)] = db("SIG%d" % o + tg)[0]
                                forward(st, hc.SIG[o], A, cons_s)
                                Csb = sb(st, "Csb", [R, 2, 128, 64], BF16)
                                ych = sb(st, "ych", [64, A, 128], F32)
                                Hc = [sb(st, "Hc%d" % i, [R, 8, 2, A], BF16) for i in range(2)]
                                for c2 in range(32):
                                    pb = PS[4 + c2 % 2]
                                    for e_ in range(2):
                                        c = c2 * 2 + e_
                                        mm(pb.t[0:R, e_ * 256:(e_ + 1) * 256], Y.t[:, 0, c, :], fic.t[:, 0, :], True, False, [Y.b, fic.b], [pb.b])
                                        mm(pb.t[0:R, e_ * 256:(e_ + 1) * 256], Y.t[:, 1, c, :], fic.t[:, 1, :], False, True, [Y.b, fic.b], [pb.b])
                                    cp("act" if c2 % 2 == 0 else "dve",
                                       Csb.t[:, :, :, c2 * 2:c2 * 2 + 2].rearrange("r e t c -> r c e t"),
                                       pb.t[0:R, 0:512].rearrange("r (c e t) -> r c e t", c=2, e=2), [pb.b], [Csb.b])
                                for t0_ in range(0, 128, 8):
                                    h_ = Hc[(t0_ // 8) % 2]
                                    dma("act", h_.t[:], hc.HB[:, t0_:t0_ + 8, :, :], db("HB" + tg), [h_.b])
                                    pb = PS[6 + (t0_ // 8) % 2]
                                    for jj in range(8):
                                        tl = t0_ + jj
                                        mm(pb.t[0:64, jj * A:(jj + 1) * A], Csb.t[0:R, 0, tl, :], h_.t[0:R, jj, 0, :], True, False, [Csb.b, h_.b], [pb.b])
                                        mm(pb.t[0:64, jj * A:(jj + 1) * A], Csb.t[0:R, 1, tl, :], h_.t[0:R, jj, 1, :], False, True, [Csb.b, h_.b], [pb.b])
                                    cp("act" if (t0_ // 8) % 2 == 0 else "dve",
                                       ych.t[:, :, t0_:t0_ + 8].rearrange("c a t -> c t a"),
                                       pb.t[0:64, 0:8 * A].rearrange("c (t a) -> c t a", t=8), [pb.b], [ych.b])
                                ychf = ych.t[:, :, :].rearrange("c a t -> c (a t)")
                                uu = [sb(st, "uu%d" % i, [64, 1024], F32) for i in range(2)]
                                gg = [sb(st, "gg%d" % i, [64, 1024], F32) for i in range(2)]
                                rb = [sb(st, "rb%d" % i, [64, 1024], BF16) for i in range(2)]
                                gw = min(1024, n)
                                for q_ in range(n // gw):
                                    u_, g_, r_ = uu[q_ % 2], gg[q_ % 2], rb[q_ % 2]
                                    sl = slice(q_ * gw, (q_ + 1) * gw)
                                    if o == 0:
                                        dma("sp", u_.t[:, :gw], hc.DWC[0, :, sl], db("DWC" + tg), [u_.b])
                                    else:
                                        dma("sp", u_.t[:, :gw], hc.ZF[:, sl], db("ZF" + tg), [u_.b])
                                    dma("act", g_.t[:, :gw], hc.DWC[1 + o, :, sl], db("DWC" + tg), [g_.b])
                                    stt(u_.t[:, :gw], u_.t[:, :gw], hb64.t[:, l, o * 4 + g:o * 4 + g + 1], ychf[:, sl], ALU.mult, ALU.add,
                                        [u_.b, hb64.b, ych.b], [u_.b])
                                    tt("dve", u_.t[:, :gw], u_.t[:, :gw], g_.t[:, :gw], ALU.mult, [u_.b, g_.b], [u_.b])
                                    if o == 0:
                                        dma("sp", hc.ZF[:, sl], u_.t[:, :gw], [u_.b], db("ZF" + tg))
                                        cp("act", r_.t[:, :gw], u_.t[:, :gw], [u_.b], [r_.b])
                                        dma("sp", hc.SIG[1][:, sl], r_.t[:, :gw], [r_.b], db("SIG1" + tg))
                                    else:
                                        c0y = t0c + q_ * gw
                                        dma("sp", YT[256 + g * 64:256 + (g + 1) * 64, c0y:c0y + gw], u_.t[:, :gw], [u_.b],
                                            db("YT", c0y, c0y + gw))
                                S.flush()

            if not with_hyena:
                with ExitStack() as st:
                    zf = sb(st, "zf", [128, 512], F32)
                    S.op("pool", lambda e: e.memset(zf.t[:], 0.0), [], [zf.b])
                    for (t0, n) in BLOCKS:
                        if t0 == 0 and not need_ctx:
                            continue
                        for r in (256, 384):
                            dma("sp", YT[r:r + 128, t0:t0 + n], zf.t[:, :n], [zf.b], db("YT", t0, t0 + n))
                    S.flush()

            with ExitStack() as st:
                wout = sb(st, "wout", [128, 8, D], BF16)
                for k in range(8):
                    dma("pool", wout.t[:, k, :], w_out[l, k * 128:(k + 1) * 128, :], [], [wout.b])
                yb = [sb(st, "yb%d" % i, [128, 8, 512], F32) for i in range(2)]
                xb = [sb(st, "xb%d" % i, [128, 8, 512], F32) for i in range(2)]
                xm = [sb(st, "xm%d" % i, [128, 8, 512], F32) for i in range(2)]
                ysq = sb(st, "ysq", [128, 8, 512], BF16)
                yn = sb(st, "yn", [128, 8, 512], BF16)
                lng = sb(st, "lng", [128, 512], F32)
                rg = [sb(st, "rg%d" % i, [128, 512], F32) for i in range(6)]
                sq = sb(st, "sq", [128, 8, 512], BF16)
                hT = [sb(st, "hT%d" % i, [128, 8, 512], BF16) for i in range(2)]
                tmp = [sb(st, "tmp%d" % i, [128, 512], F32) for i in range(2)]
                lnt = sb(st, "lnt", [128, 512], F32)
                rstd = sb(st, "rstd", [128, 512], F32)
                groups = [(0, 1), (2, 3), (4,), (5,), (6,), (7,)]
                gof = [0, 0, 1, 1, 2, 3, 4, 5]
                pc = 0
                for bi, (t0, n) in enumerate(BLOCKS):
                    if t0 == 0 and not need_ctx:
                        continue
                    j = 1 if t0 == 0 else 0
                    yb_, xb_, xm_, hT_ = yb[bi % 2], xb[bi % 2], xm[bi % 2], hT[bi % 2]
                    dma("sp", yb_.t[:, :, :n], YTv[:, :, t0:t0 + n], db("YT", t0, t0 + n), [yb_.b])
                    dma("act", xb_.t[:, :, :n], XTv[:, :, t0:t0 + n], db("XT", t0, t0 + n), [xb_.b])
                    act(ysq.t[:, :, :n], yb_.t[:, :, :n], AF.Square, [yb_.b], [ysq.b])
                    for gi, tiles in enumerate(groups):
                        pg = PS[1 + gi % 2]
                        for ii, t_ in enumerate(tiles):
                            mm(pg.t[:, :n], ones_bf.t[:], ysq.t[:, t_, :n], ii == 0, ii == len(tiles) - 1, [ones_bf.b, ysq.b], [pg.b])
                        rstd_from_psum(pg.t[:, :n], lng, rg[gi], n, 1.0 / (128 * len(tiles)), pg.b)
                    for t_ in range(8):
                        stt(yn.t[:, t_, :n], yb_.t[:, t_, :n], gout2.t[:, t_:t_ + 1], rg[gof[t_]].t[:, :n], ALU.mult, ALU.mult,
                            [yb_.b, gout2.b, rg[gof[t_]].b], [yn.b])
                    for m in range(8):
                        po = PS[3 + pc % 2]
                        pc += 1
                        for t_ in range(8):
                            mm(po.t[:, :n], wout.t[:, t_, m * 128:(m + 1) * 128], yn.t[:, t_, :n], t_ == 0, t_ == 7, [wout.b, yn.b], [po.b])
                        pf = tmp[m % 2]
                        cp("act", pf.t[:, :n], po.t[:, :n], [po.b], [pf.b])
                        stt(xm_.t[:, m, :n], pf.t[:, :n], modT.t[:, 16 + m, j:j + 1], xb_.t[:, m, :n], ALU.mult, ALU.add,
                            [pf.b, modT.b, xb_.b], [xm_.b])
                    dma("sp", XTv[:, :, t0:t0 + n], xm_.t[:, :, :n], [xm_.b], db("XT", t0, t0 + n))
                    norm_mod(xm_, n, j, G2, 24, sq, lnt, rstd, tmp, hT_, PS[0])
                    pc0 = 1 if t0 == 0 else LB + (t0 - 256)
                    dma("act", H2Tv[:, :, pc0:pc0 + n], hT_.t[:, :, :n], [hT_.b], db("H2T", pc0, pc0 + n))
                S.flush()
            if stop_after == "E1":
                break

            with ExitStack() as st:
                wup = sb(st, "wup", [128, 8, 2 * DFF], BF16)
                for k in range(8):
                    for c0 in range(0, 2 * DFF, 1024):
                        c1 = min(c0 + 1024, 2 * DFF)
                        dma("pool", wup.t[:, k, c0:c1], w_up[l, k * 128:(k + 1) * 128, c0:c1], [], [wup.b])
                hb = [sb(st, "hb%d" % i, [128, 8, 512], BF16) for i in range(2)]
                asb = [sb(st, "asb%d" % i, [128, 512], F32) for i in range(2)]
                c1t = [sb(st, "c1t%d" % i, [128, 512], F32) for i in range(2)]
                c2t = [sb(st, "c2t%d" % i, [128, 512], F32) for i in range(2)]
                gt = [sb(st, "gt%d" % i, [128, 512], F32) for i in range(2)]
                s2t = [sb(st, "s2t%d" % i, [128, 512], F32) for i in range(2)]
                ut = [sb(st, "ut%d" % i, [128, 512], BF16) for i in range(2)]
                cc = 0
                for bi, (c0, n, tok0, _) in enumerate(FFN_BLOCKS):
                    if c0 == 0 and not need_ctx:
                        continue
                    hb_ = hb[bi % 2]
                    dma("sp", hb_.t[:, :, :n], H2Tv[:, :, c0:c0 + n], db("H2T", c0, c0 + n), [hb_.b])
                    nv = n - 2
                    for cb in range(22):
                        i2 = cc % 2
                        cc += 1
                        pa, pv = PS[i2], PS[2 + i2]
                        for k in range(8):
                            mm(pa.t[:, :n], wup.t[:, k, cb * 128:(cb + 1) * 128], hb_.t[:, k, :n], k == 0, k == 7, [wup.b, hb_.b], [pa.b])
                        for k in range(8):
                            mm(pv.t[:, :n], wup.t[:, k, DFF + cb * 128:DFF + (cb + 1) * 128], hb_.t[:, k, :n], k == 0, k == 7,
                               [wup.b, hb_.b], [pv.b])
                        a_ = asb[i2]
                        cp("act", a_.t[:, :n], pa.t[:, :n], [pa.b], [a_.b])
                        ts("dve", c1t[i2].t[:, :nv], a_.t[:, 1:n - 1], fcw.t[:, l, 1, cb:cb + 1], ALU.mult, [a_.b, fcw.b, fcb.b], [c1t[i2].b],
                           s2=fcb.t[:, l, cb:cb + 1], op1=ALU.add)
                        stt(c2t[i2].t[:, :nv], a_.t[:, 0:n - 2], fcw.t[:, l, 0, cb:cb + 1], c1t[i2].t[:, :nv], ALU.mult, ALU.add,
                            [a_.b, fcw.b, c1t[i2].b], [c2t[i2].b])
                        stt(c1t[i2].t[:, :nv], a_.t[:, 2:n], fcw.t[:, l, 2, cb:cb + 1], c2t[i2].t[:, :nv], ALU.mult, ALU.add,
                            [a_.b, fcw.b, c2t[i2].b], [c1t[i2].b])
                        xx = c1t[i2]
                        tt("dve", s2t[i2].t[:, :nv], xx.t[:, :nv], xx.t[:, :nv], ALU.mult, [xx.b], [s2t[i2].b])
                        ts("dve", s2t[i2].t[:, :nv], s2t[i2].t[:, :nv], 0.044715, ALU.mult, [s2t[i2].b], [s2t[i2].b], s2=1.0, op1=ALU.add)
                        tt("dve", s2t[i2].t[:, :nv], s2t[i2].t[:, :nv], xx.t[:, :nv], ALU.mult, [s2t[i2].b, xx.b], [s2t[i2].b])
                        act(gt[i2].t[:, :nv], s2t[i2].t[:, :nv], AF.Sigmoid, [s2t[i2].b], [gt[i2].b], scale=1.5957691216057308)
                        tt("dve", gt[i2].t[:, :nv], gt[i2].t[:, :nv], xx.t[:, :nv], ALU.mult, [gt[i2].b, xx.b], [gt[i2].b])
                        cp("act", c2t[i2].t[:, :nv], pv.t[:, 1:n - 1], [pv.b], [c2t[i2].b])
                        tt("dve", ut[i2].t[:, :nv], gt[i2].t[:, :nv], c2t[i2].t[:, :nv], ALU.mult, [gt[i2].b, c2t[i2].b], [ut[i2].b])
                        dma("sp", UT[cb * 128:(cb + 1) * 128, c0 + 1:c0 + 1 + nv], ut[i2].t[:, :nv], [ut[i2].b], db("UT", c0 + 1, c0 + 1 + nv))
                S.flush()

            last = (l == n_layers - 1)
            with ExitStack() as st:
                wdn = sb(st, "wdn", [128, 22, D], BF16)
                for k in range(22):
                    dma("pool", wdn.t[:, k, :], w_down[l, k * 128:(k + 1) * 128, :], [], [wdn.b])
                ub = [sb(st, "ub%d" % i, [128, 22, 512], BF16) for i in range(2)]
                xb = [sb(st, "xb%d" % i, [128, 8, 512], F32) for i in range(2)]
                xo = [sb(st, "xo%d" % i, [128, 8, 512], F32) for i in range(2)]
                pft = [sb(st, "pft%d" % i, [128, 512], F32) for i in range(2)]
                pc = 0
                for bi, (c0, n, tok0, _) in enumerate(FFN_BLOCKS):
                    if c0 == 0 and not need_ctx:
                        continue
                    j = 1 if c0 == 0 else 0
                    nv = n - 2
                    ub_, xb_, xo_ = ub[bi % 2], xb[bi % 2], xo[bi % 2]
                    dma("sp", ub_.t[:, :, :nv], UTv[:, :, c0 + 1:c0 + 1 + nv], db("UT", c0 + 1, c0 + 1 + nv), [ub_.b])
                    dma("act", xb_.t[:, :, :nv], XTv[:, :, tok0:tok0 + nv], db("XT", tok0, tok0 + nv), [xb_.b])
                    for m in range(8):
                        pd = PS[pc % 2]
                        pc += 1
                        for cb in range(22):
                            mm(pd.t[:, :nv], wdn.t[:, cb, m * 128:(m + 1) * 128], ub_.t[:, cb, :nv], cb == 0, cb == 21, [wdn.b, ub_.b], [pd.b])
                        pf = pft[m % 2]
                        cp("act", pf.t[:, :nv], pd.t[:, :nv], [pd.b], [pf.b])
                        stt(xo_.t[:, m, :nv], pf.t[:, :nv], modT.t[:, 40 + m, j:j + 1], xb_.t[:, m, :nv], ALU.mult, ALU.add,
                            [pf.b, modT.b, xb_.b], [xo_.b])
                    dma("sp", XTv[:, :, tok0:tok0 + nv], xo_.t[:, :, :nv], [xo_.b], db("XT", tok0, tok0 + nv))
                S.flush()

        with ExitStack() as st:
            xb = [sb(st, "xb%d" % i, [128, 8, 512], F32) for i in range(2)]
            ob = [sb(st, "ob%d" % i, [128, D], F32) for i in range(2)]
            oc = 0
            for bi in range(NQB):
                t0 = 256 + 512 * bi
                xb_ = xb[bi % 2]
                dma("sp", xb_.t[:], XTv[:, :, t0:t0 + 512], db("XT", t0, t0 + 512), [xb_.b])
                for tti in range(4):
                    o_ = ob[oc % 2]
                    oc += 1
                    for half in range(2):
                        pt = PS[(oc % 2) * 2 + half]
                        for kk in range(4):
                            k = half * 4 + kk
                            S.op("pe", lambda e, pt=pt, kk=kk, k=k, xb_=xb_, tti=tti: e.transpose(
                                out=pt.t[:, kk * 128:(kk + 1) * 128], in_=xb_.t[:, k, tti * 128:(tti + 1) * 128], identity=ident.t[:]),
                                [xb_.b, ident.b], [pt.b])
                        cp("act" if half == 0 else "dve", o_.t[:, half * 512:(half + 1) * 512], pt.t[:, :], [pt.b], [o_.b])
                    r = 512 * bi + tti * 128
                    dma("sp", out[r:r + 128, :], o_.t[:], [o_.b], db("out", r, r + 128))
            S.flush()
    return nc


def host_inputs(inputs, n_layers=L):
    f = lambda a: np.ascontiguousarray(np.asarray(a, dtype=np.float32))
    pk = lambda v: f(v).reshape(-1, 128).T
    common = {}
    ng = np.zeros((128, L, 2, 8), np.float32)
    gout = np.zeros((128, L, 8), np.float32)
    qkg = np.zeros((128, L, 4), np.float32)
    lamv = np.zeros((64, L, 4), np.float32)
    fcw = np.zeros((128, L, 3, 22), np.float32)
    fcb = np.zeros((128, L, 22), np.float32)
    for l in range(L):
        ng[:, l, 0, :] = pk(inputs["norm1_g"][l])
        ng[:, l, 1, :] = pk(inputs["norm2_g"][l])
        gout[:, l, :] = pk(inputs["g_out"][l])
        for gi, nm in enumerate(["qn_a", "kn_a", "qn_d", "kn_d"]):
            qkg[:, l, gi] = np.tile(f(inputs[nm][l]), 2)
        for gi, nm in enumerate(["lam_q1", "lam_k1", "lam_q2", "lam_k2"]):
            lamv[:, l, gi] = f(inputs[nm][l])
        for jj in range(3):
            fcw[:, l, jj, :] = pk(inputs["ffn_conv_w"][l, jj])
        fcb[:, l, :] = pk(inputs["ffn_conv_b"][l])
    common.update(ng=ng, gout=gout, qkg=qkg, lamv=lamv, fcw=fcw, fcb=fcb)
    common["bmod"] = f(inputs["b_mod"]).reshape(1, -1)
    for nm in ["w_mod", "w_in", "w_out", "w_up", "w_down"]:
        common[nm] = f(inputs[nm][:n_layers])
    common["ident"] = np.eye(128, dtype=np.float32)
    bd = np.zeros((128, 128), np.float32)
    bd[:64, :64] = 1.0
    bd[64:, 64:] = 1.0
    common["onesbd"] = bd
    rot = np.zeros((128, 128), np.float32)
    for i in range(64):
        rot[2 * i + 1, 2 * i] = -1.0
        rot[2 * i, 2 * i + 1] = 1.0
    common["rot"] = rot
    pos = np.arange(NLAT)
    rows = (pos // 64).astype(np.float32)
    cols = (pos % 64).astype(np.float32)
    inv = (np.float32(10000.0) ** (-np.arange(16, dtype=np.float32) / np.float32(16))).astype(np.float32)
    ang = np.concatenate([rows[:, None] * inv, cols[:, None] * inv], axis=-1).astype(np.float32)
    cos_t = np.ones((128, NT), np.float32)
    sin_t = np.zeros((128, NT), np.float32)
    pidx = (np.arange(128) % 64) // 2
    cos_t[:, NCTX:] = np.cos(ang).T[pidx]
    sin_t[:, NCTX:] = np.sin(ang).T[pidx]
    common["cos_t"] = cos_t
    common["sin_t"] = sin_t

    hp64 = np.zeros((64, L, 3), np.float32)
    fb3_64 = np.zeros((64, L, 16), np.float32)
    hb64 = np.zeros((64, L, 8), np.float32)
    hcw64 = np.zeros((64, L, 3, 12), np.float32)
    hcb64 = np.zeros((64, L, 12), np.float32)
    for l in range(L):
        hp64[:, l, 0] = f(inputs["hy_freq"][l])
        hp64[:, l, 1] = f(inputs["hy_fb1"][l])
        hp64[:, l, 2] = f(inputs["hy_fb2"][l])
        fb3_64[:, l, :] = f(inputs["hy_fb3"][l]).reshape(16, 64).T
        hb64[:, l, :] = f(inputs["hy_bias"][l]).reshape(8, 64).T
        for jj in range(3):
            hcw64[:, l, jj, :] = f(inputs["hy_conv_w"][l, jj]).reshape(12, 64).T
        hcb64[:, l, :] = f(inputs["hy_conv_b"][l]).reshape(12, 64).T
    common.update(hp64=hp64, fb3_64=fb3_64, hb64=hb64, hcw64=hcw64, hcb64=hcb64)
    for nm in ["hy_fw1", "hy_fw2", "hy_fw3"]:
        common[nm] = f(inputs[nm][:n_layers])
    max_decay = math.log(1e-2) / 0.3
    min_decay = math.log(1e-2) / 1.5
    deltas = np.linspace(min_decay, max_decay, 256, dtype=np.float32)
    common["dsc64"] = np.ascontiguousarray((-np.abs(deltas)).reshape(4, 64).T.astype(np.float32))
    tt_ = np.arange(128, dtype=np.float64)
    pp_ = np.arange(128, dtype=np.float64)
    ang = 2.0 * np.pi * np.outer(pp_, tt_) / 128.0
    fic = np.zeros((128, 2, 256), np.float32)
    fic[:, 0, :128] = np.cos(ang)
    fic[:, 0, 128:] = np.sin(ang)
    fic[:, 1, :128] = -np.sin(ang)
    fic[:, 1, 128:] = np.cos(ang)
    common["ficat"] = fic
    for n_ in (NLAT, NCTX):
        R_, A_, M_ = n_ // 64, n_ // 128, 2 * n_
        s_ = np.arange(M_)
        posi = np.where(s_ < n_, s_, (2 * n_ - s_) % n_).astype(np.float32)
        tq = (posi / np.float32(max(n_ - 1, 1))).astype(np.float32)
        wq = (np.float32(2.0 * math.pi) * posi / np.float32(n_)).astype(np.float32)
        bands = np.linspace(1e-4, 15, 16, dtype=np.float32)
        z = np.concatenate([tq[:, None], np.cos(bands[None, :] * wq[:, None]), -np.sin(bands[None, :] * wq[:, None])], axis=-1)
        common["zext%d" % n_] = np.ascontiguousarray(z.T.astype(np.float32))
        common["tpos%d" % n_] = np.ascontiguousarray(tq.reshape(1, M_))
        a_ = np.arange(R_, dtype=np.float64)
        r_ = np.arange(R_, dtype=np.float64)
        an = 2.0 * np.pi * np.outer(a_, r_) / R_
        common["f1cat%d" % n_] = np.concatenate([np.cos(an), -np.sin(an)], axis=1).astype(np.float32)
        b_ = np.arange(128, dtype=np.float64)
        p_ = np.arange(128, dtype=np.float64)
        Gm = np.zeros((128, R_, 3, 128), np.float32)
        for r in range(R_):
            th = 2.0 * np.pi * (np.outer(b_, p_) / 128.0 + (b_ * r / M_)[:, None])
            Gm[:, r, 0, :] = np.cos(th)
            Gm[:, r, 1, :] = -np.sin(th)
            Gm[:, r, 2, :] = np.sin(th)
        common["Gm%d" % n_] = Gm
        Hm = np.zeros((R_, 128, 2, A_), np.float32)
        th_ = np.arange(A_, dtype=np.float64)
        for tl in range(128):
            ph = 2.0 * np.pi * np.outer(r_, 128.0 * th_ + tl) / M_
            Hm[:, tl, 0, :] = np.cos(ph) / M_
            Hm[:, tl, 1, :] = -np.sin(ph) / M_
        common["Hm%d" % n_] = Hm
    maps = []
    for b in range(NCORES):
        m = dict(common)
        m["x"] = f(inputs["x"][b])
        m["ctx"] = f(inputs["ctx"][b])
        cv = np.zeros((128, 8, 2), np.float32)
        cv[:, :, 0] = pk(inputs["c"][b])
        cv[:, :, 1] = pk(inputs["c_ctx"])
        m["cvec"] = cv
        maps.append(m)
    return maps


_NC_CACHE = {}


def kernel(**inputs):
    maps = host_inputs(inputs)
    if "nc" not in _NC_CACHE:
        _NC_CACHE["nc"] = build()
    nc = _NC_CACHE["nc"]
    res = run_bass_kernel_spmd(nc, maps, core_ids=list(range(NCORES)))
    return np.stack([np.asarray(r["out"], dtype=np.float32) for r in res.results], axis=0)
```
